# Optimizing a Trainium2 kernel written in Bass

```python
import jax, jax.numpy as jnp
from jax import lax
import numpy as np

D_MODEL = 1024
BATCH = 8
SEQ = 2048
DEPTH = 4

GRID_W = 64
CTX_LEN = 256
EPS = 1e-6
N_MOD = 6

RW_HEAD = 64
RW_HEADS = D_MODEL // RW_HEAD
RW_DIM = RW_HEADS * RW_HEAD
DECAY_LORA = 64
ICLR_LORA = 64
GATE_LORA = 128
RW_GN_EPS = 64e-5
RW_SPLIT_POINTS = (RW_DIM, 2 * RW_DIM, 3 * RW_DIM,
                   3 * RW_DIM + DECAY_LORA, 3 * RW_DIM + 2 * DECAY_LORA,
                   3 * RW_DIM + 2 * DECAY_LORA + ICLR_LORA, 3 * RW_DIM + 2 * DECAY_LORA + 2 * ICLR_LORA)
RWKV_COLS = 3 * RW_DIM + 2 * DECAY_LORA + 2 * ICLR_LORA + GATE_LORA

CONV_DIM = D_MODEL
CONV_WIDTH = 31
CONV_COLS = 2 * CONV_DIM

ATT_HEAD = 128
ATT_Q_HEADS = D_MODEL // ATT_HEAD
ATT_KV_HEADS = 2
ATT_GROUP = ATT_Q_HEADS // ATT_KV_HEADS
Q_COLS = ATT_Q_HEADS * ATT_HEAD
KV_COLS = ATT_KV_HEADS * ATT_HEAD
ATT_COLS = Q_COLS + 2 * KV_COLS
ROPE_THETA = 10000.0
Q_BLOCK = 128

N_BRANCH = 3
GATE_COLS = N_BRANCH * D_MODEL
IN_SPLIT_POINTS = (RWKV_COLS, RWKV_COLS + CONV_COLS, RWKV_COLS + CONV_COLS + ATT_COLS)
IN_COLS = RWKV_COLS + CONV_COLS + ATT_COLS + GATE_COLS

D_FF = 4 * D_MODEL

kernel_name = 'hybrid_rwkv7_conformer_gqa_dit_block'


def rms_norm(x, g):
    xf = x.astype(jnp.float32)
    y = xf * lax.rsqrt(jnp.mean(xf * xf, axis=-1, keepdims=True) + EPS)
    return (y * g.astype(jnp.float32)).astype(x.dtype)


def layer_norm(x, g, b):
    xf = x.astype(jnp.float32)
    mu = jnp.mean(xf, axis=-1, keepdims=True)
    var = jnp.mean(jnp.square(xf - mu), axis=-1, keepdims=True)
    y = (xf - mu) * lax.rsqrt(var + EPS)
    return (y * g.astype(jnp.float32) + b.astype(jnp.float32)).astype(x.dtype)


def modulate(x, g, shift, scale):
    return rms_norm(x, g) * (1 + scale) + shift


def centred_shift(z):
    zp = jnp.pad(z, ((0, 0), (1, 1), (0, 0)))
    return 0.5 * (zp[:, :-2] + zp[:, 2:])


def depthwise_conv(z, w, b):
    out = lax.conv_general_dilated(
        z, w[:, None, :], window_strides=(1,),
        padding=((CONV_WIDTH // 2, CONV_WIDTH // 2),),
        dimension_numbers=('NWC', 'WIO', 'NWC'),
        feature_group_count=z.shape[-1])
    return out + b


def axial_rope_tables(rows):
    row = jnp.repeat(jnp.arange(rows), GRID_W).astype(jnp.float32)
    col = jnp.tile(jnp.arange(GRID_W), rows).astype(jnp.float32)
    axis_dim = ATT_HEAD // 2
    freqs = ROPE_THETA ** (-jnp.arange(0, axis_dim, 2, dtype=jnp.float32) / axis_dim)
    ang = jnp.concatenate([row[:, None] * freqs, col[:, None] * freqs], axis=-1)
    return jnp.cos(ang), jnp.sin(ang)


def apply_rope(x, cos, sin):
    shape = (cos.shape[0],) + (1,) * (x.ndim - 3) + (cos.shape[1],)
    cos = cos.reshape(shape)
    sin = sin.reshape(shape)
    xf = x.astype(jnp.float32)
    x1, x2 = xf[..., 0::2], xf[..., 1::2]
    out = jnp.stack([x1 * cos - x2 * sin, x1 * sin + x2 * cos], axis=-1).reshape(x.shape)
    return out.astype(x.dtype)


def sdpa(q, k, v):
    s = jnp.einsum('bqhgd,bkhd->bhgqk', q, k).astype(jnp.float32) * (ATT_HEAD ** -0.5)
    p = jax.nn.softmax(s, axis=-1).astype(v.dtype)
    return jnp.einsum('bhgqk,bkhd->bqhgd', p, v)


def _rwkv_inputs(z, mu, w0, w2, a0, a2, g2, k_k, k_a):
    B, L, _ = z.shape
    z = z + mu * (centred_shift(z) - z)
    r, k, v, wd_f, wd_b, ad_f, ad_b, gd = jnp.split(z, RW_SPLIT_POINTS, axis=-1)

    def heads(t):
        return t.astype(jnp.float32).reshape(B, L, RW_HEADS, RW_HEAD)

    kk = heads(k * k_k)
    kk = kk / jnp.maximum(jnp.sqrt(jnp.sum(kk * kk, axis=-1, keepdims=True)), 1e-12)
    decays, keys, iclrs = [], [], []
    for d, (wd, ad) in enumerate(((wd_f, ad_f), (wd_b, ad_b))):
        w_raw = (w0[d] + jnp.tanh(wd) @ w2[d]).astype(jnp.float32)
        decays.append(heads(jnp.exp(-jnp.exp(-jax.nn.softplus(-w_raw) - 0.5))))
        a = jax.nn.sigmoid((a0[d] + ad @ a2[d]).astype(jnp.float32))
        keys.append(heads(k.astype(jnp.float32) * (1.0 + (a - 1.0) * k_a.astype(jnp.float32))))
        iclrs.append(heads(a))
    g = jax.nn.sigmoid(gd) @ g2
    return heads(r), heads(v), kk, g, decays, keys, iclrs


def _rwkv_scan(state0, r, decay, k, v, kk, a, reverse):
    seq = tuple(jnp.swapaxes(t, 0, 1) for t in (r, decay, k, v, kk, a))

    def step(S, inp):
        r_t, w_t, k_t, v_t, kk_t, a_t = inp
        s_kk = jnp.einsum('bhij,bhj->bhi', S, kk_t)
        S = (S * w_t[:, :, None, :]
             - s_kk[..., None] * (kk_t * a_t)[:, :, None, :]
             + v_t[..., None] * k_t[:, :, None, :])
        return S, jnp.einsum('bhij,bhj->bhi', S, r_t)

    S, y = lax.scan(step, state0, seq, reverse=reverse)
    return S, jnp.swapaxes(y, 0, 1)


def _rwkv_readout(y, bonus, g, ln_g, ln_b):
    B, L = y.shape[:2]
    mu = jnp.mean(y, axis=-1, keepdims=True)
    var = jnp.mean(jnp.square(y - mu), axis=-1, keepdims=True)
    yn = ((y - mu) * lax.rsqrt(var + RW_GN_EPS)).reshape(B, L, RW_DIM)
    yn = yn * ln_g.astype(jnp.float32) + ln_b.astype(jnp.float32)
    return ((yn + bonus.reshape(B, L, RW_DIM)) * g.astype(jnp.float32)).astype(g.dtype)


def rwkv_branch(zc, zl, mu, w0, w2, a0, a2, g2, k_k, k_a, r_k, ln_g, ln_b, ctx_out):
    prm = (mu, w0, w2, a0, a2, g2, k_k, k_a)
    rc, vc, kkc, gc, dec_c, key_c, icl_c = _rwkv_inputs(zc, *prm)
    rl, vl, kkl, gl, dec_l, key_l, icl_l = _rwkv_inputs(zl, *prm)
    r_k = r_k.astype(jnp.float32)
    state0 = jnp.zeros((zl.shape[0], RW_HEADS, RW_HEAD, RW_HEAD), jnp.float32)
    y_c = y_l = b_c = b_l = 0.0
    for d in range(2):
        rev = d == 1
        state_ctx, yc_d = _rwkv_scan(state0, rc, dec_c[d], key_c[d], vc, kkc, icl_c[d], rev)
        _, yl_d = _rwkv_scan(state_ctx, rl, dec_l[d], key_l[d], vl, kkl, icl_l[d], rev)
        y_c = y_c + yc_d
        y_l = y_l + yl_d
        b_c = b_c + jnp.sum(rc * key_c[d] * r_k, axis=-1, keepdims=True) * vc
        b_l = b_l + jnp.sum(rl * key_l[d] * r_k, axis=-1, keepdims=True) * vl
    out_l = _rwkv_readout(y_l, b_l, gl, ln_g, ln_b)
    out_c = _rwkv_readout(y_c, b_c, gc, ln_g, ln_b) if ctx_out else None
    return out_c, out_l


def conv_branch(z, dw_w, dw_b, ln_g, ln_b):
    a, b = jnp.split(z, 2, axis=-1)
    u = depthwise_conv(a * jax.nn.sigmoid(b), dw_w, dw_b)
    return jax.nn.silu(layer_norm(u, ln_g, ln_b))


def attention_branch(zc, zl, q_g, k_g, cos, sin, ctx_out):
    def split_heads(z):
        B, L, _ = z.shape
        q, k, v = jnp.split(z, (Q_COLS, Q_COLS + KV_COLS), axis=-1)
        q = rms_norm(q.reshape(B, L, ATT_KV_HEADS, ATT_GROUP, ATT_HEAD), q_g)
        k = rms_norm(k.reshape(B, L, ATT_KV_HEADS, ATT_HEAD), k_g)
        return q, k, v.reshape(B, L, ATT_KV_HEADS, ATT_HEAD)

    qc, kc, vc = split_heads(zc)
    ql, kl, vl = split_heads(zl)
    ql = apply_rope(ql, cos, sin)
    kl = apply_rope(kl, cos, sin)
    k_all = jnp.concatenate([kc, kl], axis=1)
    v_all = jnp.concatenate([vc, vl], axis=1)
    B, S = zl.shape[:2]
    n_blocks = S // Q_BLOCK
    q_blocks = jnp.swapaxes(ql.reshape(B, n_blocks, Q_BLOCK, ATT_KV_HEADS, ATT_GROUP, ATT_HEAD), 0, 1)
    o_blocks = lax.map(lambda qb: sdpa(qb, k_all, v_all), q_blocks)
    out_l = jnp.swapaxes(o_blocks, 0, 1).reshape(B, S, Q_COLS)
    if not ctx_out:
        return None, out_l
    out_c = sdpa(qc, kc, vc).reshape(B, zc.shape[1], Q_COLS)
    return out_c, out_l


def merge_branches(o_rw, o_cv, o_at, z_gate, w_branch, w_out):
    branches = jnp.stack([o_rw, o_cv.astype(o_rw.dtype), o_at.astype(o_rw.dtype)], axis=-2)
    proj = jnp.einsum('blnc,ncd->blnd', branches, w_branch)
    gates = jax.nn.sigmoid(z_gate.reshape(proj.shape).astype(jnp.float32)).astype(proj.dtype)
    return jnp.sum(gates * proj, axis=-2) @ w_out


def token_mixer(hc, hl, cos, sin, w_in, rw, cv, at, w_branch, w_out, ctx_out):
    zc = hc @ w_in
    zl = hl @ w_in
    rwc, cvc, atc, gtc = jnp.split(zc, IN_SPLIT_POINTS, axis=-1)
    rwl, cvl, atl, gtl = jnp.split(zl, IN_SPLIT_POINTS, axis=-1)
    o_rw_c, o_rw_l = rwkv_branch(rwc, rwl, *rw, ctx_out)
    o_at_c, o_at_l = attention_branch(atc, atl, *at, cos, sin, ctx_out)
    o_cv_l = conv_branch(cvl, *cv)
    out_l = merge_branches(o_rw_l, o_cv_l, o_at_l, gtl, w_branch, w_out)
    if not ctx_out:
        return None, out_l
    o_cv_c = conv_branch(cvc, *cv)
    out_c = merge_branches(o_rw_c, o_cv_c, o_at_c, gtc, w_branch, w_out)
    return out_c, out_l


def sq_relu_mlp(h, w1, w2):
    return jnp.square(jax.nn.relu(h @ w1)) @ w2


def setup_inputs(seed: int = 0) -> dict:
    key = jax.random.key(seed)
    ks = jax.random.split(key, 32)
    L, D = DEPTH, D_MODEL

    def nrm(k, shape, s):
        return jax.random.normal(k, shape, jnp.float32) * s

    return {
        'x': nrm(ks[0], (BATCH, SEQ, D), 1.0),
        'c': nrm(ks[1], (BATCH, D), 1.0),
        'ctx': nrm(ks[2], (BATCH, CTX_LEN, D), 1.0),
        'c_ctx': nrm(ks[3], (D,), 1.0),
        'w_mod': nrm(ks[4], (L, D, N_MOD * D), 0.5 * D ** -0.5),
        'b_mod': nrm(ks[5], (L, N_MOD * D), 0.01),
        'norm_mix_pre': 1.0 + nrm(ks[6], (L, D), 0.05),
        'norm_mix_post': 1.0 + nrm(ks[7], (L, D), 0.05),
        'norm_mlp_pre': 1.0 + nrm(ks[8], (L, D), 0.05),
        'norm_mlp_post': 1.0 + nrm(ks[9], (L, D), 0.05),
        'w_in': nrm(ks[10], (L, D, IN_COLS), D ** -0.5),
        'rw_mu': jax.random.uniform(ks[11], (L, RWKV_COLS), jnp.float32),
        'rw_w0': nrm(ks[12], (L, 2, RW_DIM), 1.0) - 1.0,
        'rw_w2': nrm(ks[13], (L, 2, DECAY_LORA, RW_DIM), DECAY_LORA ** -0.5),
        'rw_a0': nrm(ks[14], (L, 2, RW_DIM), 0.1),
        'rw_a2': nrm(ks[15], (L, 2, ICLR_LORA, RW_DIM), ICLR_LORA ** -0.5),
        'rw_g2': nrm(ks[16], (L, GATE_LORA, RW_DIM), GATE_LORA ** -0.5),
        'rw_k_k': 0.85 + nrm(ks[17], (L, RW_DIM), 0.05),
        'rw_k_a': 1.0 + nrm(ks[18], (L, RW_DIM), 0.05),
        'rw_r_k': nrm(ks[19], (L, RW_HEADS, RW_HEAD), 0.1),
        'rw_ln_g': 1.0 + nrm(ks[20], (L, RW_DIM), 0.05),
        'rw_ln_b': nrm(ks[21], (L, RW_DIM), 0.01),
        'cv_dw_w': nrm(ks[22], (L, CONV_WIDTH, CONV_DIM), CONV_WIDTH ** -0.5),
        'cv_dw_b': nrm(ks[23], (L, CONV_DIM), 0.01),
        'cv_ln_g': 1.0 + nrm(ks[24], (L, CONV_DIM), 0.05),
        'cv_ln_b': nrm(ks[25], (L, CONV_DIM), 0.01),
        'at_q_norm': 1.0 + nrm(ks[26], (L, ATT_HEAD), 0.05),
        'at_k_norm': 1.0 + nrm(ks[27], (L, ATT_HEAD), 0.05),
        'w_branch': nrm(ks[28], (L, N_BRANCH, D, D), D ** -0.5),
        'w_out': nrm(ks[29], (L, D, D), D ** -0.5),
        'w_ff1': nrm(ks[30], (L, D, D_FF), D ** -0.5),
        'w_ff2': nrm(ks[31], (L, D_FF, D), D_FF ** -0.5),
    }


def reference(x, c, ctx, c_ctx, w_mod, b_mod, norm_mix_pre, norm_mix_post, norm_mlp_pre, norm_mlp_post,
              w_in, rw_mu, rw_w0, rw_w2, rw_a0, rw_a2, rw_g2, rw_k_k, rw_k_a, rw_r_k, rw_ln_g, rw_ln_b,
              cv_dw_w, cv_dw_b, cv_ln_g, cv_ln_b, at_q_norm, at_k_norm, w_branch, w_out, w_ff1, w_ff2):
    rows = x.shape[1] // GRID_W
    cos, sin = axial_rope_tables(rows)
    silu_c = jax.nn.silu(c)[:, None, :]
    silu_cc = jax.nn.silu(c_ctx)
    xc, xl = ctx, x
    for l in range(DEPTH):
        ctx_out = l < DEPTH - 1
        mod_l = jnp.split(silu_c @ w_mod[l] + b_mod[l], N_MOD, axis=-1)
        mod_c = jnp.split(silu_cc @ w_mod[l] + b_mod[l], N_MOD, axis=-1)
        rw = (rw_mu[l], rw_w0[l], rw_w2[l], rw_a0[l], rw_a2[l], rw_g2[l], rw_k_k[l], rw_k_a[l],
              rw_r_k[l], rw_ln_g[l], rw_ln_b[l])
        cv = (cv_dw_w[l], cv_dw_b[l], cv_ln_g[l], cv_ln_b[l])
        at = (at_q_norm[l], at_k_norm[l])
        hc = modulate(xc, norm_mix_pre[l], mod_c[0], mod_c[1])
        hl = modulate(xl, norm_mix_pre[l], mod_l[0], mod_l[1])
        oc, ol = token_mixer(hc, hl, cos, sin, w_in[l], rw, cv, at, w_branch[l], w_out[l], ctx_out)
        xl = xl + mod_l[2] * rms_norm(ol, norm_mix_post[l])
        hl = modulate(xl, norm_mlp_pre[l], mod_l[3], mod_l[4])
        xl = xl + mod_l[5] * rms_norm(sq_relu_mlp(hl, w_ff1[l], w_ff2[l]), norm_mlp_post[l])
        if ctx_out:
            xc = xc + mod_c[2] * rms_norm(oc, norm_mix_post[l])
            hc = modulate(xc, norm_mlp_pre[l], mod_c[3], mod_c[4])
            xc = xc + mod_c[5] * rms_norm(sq_relu_mlp(hc, w_ff1[l], w_ff2[l]), norm_mlp_post[l])
    return xl
```

```python
from concourse.bass_utils import run_bass_kernel_spmd
import numpy as np
import concourse.bass as bass
import concourse.mybir as mybir
from contextlib import ExitStack

F32 = mybir.dt.float32
BF16 = mybir.dt.bfloat16
ALU = mybir.AluOpType
AF = mybir.ActivationFunctionType
AX = mybir.AxisListType

ENGS = ['tensor', 'vector', 'scalar', 'gpsimd', 'sync']
CAP = 4000
NDSEM = 8


class Ev:
    __slots__ = ('eng', 'seq', 'sem', 'val', 'clock', 'did')

    def __init__(self, eng, seq, sem, val, clock, did=None):
        self.eng = eng; self.seq = seq; self.sem = sem; self.val = val
        self.clock = clock; self.did = did


class Prog:
    def __init__(self, nc, block, sync_same=('vector', 'scalar', 'gpsimd')):
        self.nc = nc
        self.block = block
        self.es = ExitStack()
        self.scopes = []
        self.ninst = 0
        self.ops = {e: [] for e in ENGS}
        self.cnt = {e: 0 for e in ENGS}
        self.know = {e: {f: 0 for f in ENGS} for e in ENGS}
        self.dknow = {e: set() for e in ENGS}
        self.esems = {e: [] for e in ENGS}
        self.state = {}
        self.sync_same = set(sync_same)
        self.dsems = {}
        self.duse = {}
        self.dlast = {}
        self.drr = {}
        self.ndma = 0
        self.all_dma = []
        self.nwaits = 0

    def sb(self, name, shape, dtype):
        es = self.scopes[-1] if self.scopes else self.es
        self.ninst += 1
        return es.enter_context(self.nc.sbuf_tensor(f"{name}_{self.ninst}", list(shape), dtype))

    def ps(self, name, shape, dtype):
        es = self.scopes[-1] if self.scopes else self.es
        self.ninst += 1
        return es.enter_context(self.nc.psum_tensor(f"{name}_{self.ninst}", list(shape), dtype))

    def push(self):
        self.scopes.append(ExitStack())

    def pop(self):
        self.barrier()
        self.scopes.pop().close()

    def _issue(self, eng, waits, fn, inc):
        def f(e):
            for sem, val in waits:
                e.wait_ge(sem, val)
            if fn is not None:
                ins = fn(e)
                ins.then_inc(inc[0], inc[1])
        getattr(self.block, eng)(f)

    def _esem(self, eng, seq):
        i = (seq - 1) // CAP
        while len(self.esems[eng]) <= i:
            self.esems[eng].append(self.nc.alloc_semaphore(name=f"s_{eng}_{len(self.esems[eng])}"))
        return self.esems[eng][i], ((seq - 1) % CAP) + 1

    def _deps(self, r, w):
        deps = []
        for k in r:
            st = self.state.get(k)
            if st and st[0] is not None:
                deps.append(st[0])
        for k in w:
            st = self.state.get(k)
            if st:
                if st[0] is not None:
                    deps.append(st[0])
                deps.extend(st[1].values())
                deps.extend(st[2])
        return deps

    def _resolve(self, eng, deps):
        need = {}
        waits = []
        kn = self.know[eng]
        for d in deps:
            if d.did is not None:
                if d.did in self.dknow[eng]:
                    continue
                self.dknow[eng].add(d.did)
                waits.append((d.sem, d.val))
                for f, s in d.clock.items():
                    if s > kn[f]:
                        kn[f] = s
            else:
                if d.eng == eng and eng not in self.sync_same:
                    continue
                if kn[d.eng] >= d.seq:
                    continue
                if d.eng not in need or need[d.eng].seq < d.seq:
                    need[d.eng] = d
        for f, d in need.items():
            if kn[f] >= d.seq:
                continue
            waits.append((d.sem, d.val))
            for g, s in d.clock.items():
                if s > kn[g]:
                    kn[g] = s
        self.nwaits += len(waits)
        return waits

    def _update(self, ev, r, w):
        for k in w:
            self.state[k] = [ev, {}, []]
        for k in r:
            if k in w:
                continue
            st = self.state.get(k)
            if st is None:
                st = self.state[k] = [None, {}, []]
            if ev.did is not None:
                st[2].append(ev)
            else:
                st[1][ev.eng] = ev

    def op(self, eng, fn, r=(), w=()):
        waits = self._resolve(eng, self._deps(r, w))
        self.cnt[eng] += 1
        seq = self.cnt[eng]
        sem, val = self._esem(eng, seq)
        clock = dict(self.know[eng]); clock[eng] = seq
        ev = Ev(eng, seq, sem, val, clock)
        self._issue(eng, waits, fn, (sem, 1))
        self._update(ev, r, w)
        return ev

    def dma(self, q, out, in_, r=(), w=()):
        if q not in self.dsems:
            self.dsems[q] = [self.nc.alloc_semaphore(name=f"d_{q}_{i}") for i in range(NDSEM)]
            self.duse[q] = [0] * NDSEM
            self.dlast[q] = [None] * NDSEM
            self.drr[q] = 0
        waits = self._resolve(q, self._deps(r, w))
        j = self.drr[q] % NDSEM
        self.drr[q] += 1
        prev = self.dlast[q][j]
        if prev is not None and prev.did not in self.dknow[q]:
            self.dknow[q].add(prev.did)
            waits.append((prev.sem, prev.val))
        self.duse[q][j] += 1
        sem = self.dsems[q][j]; val = 16 * self.duse[q][j]
        self.ndma += 1
        ev = Ev(q, 0, sem, val, dict(self.know[q]), did=self.ndma)
        self.dlast[q][j] = ev
        self.all_dma.append(ev)
        self._issue(q, waits, (lambda e, o=out, i=in_: e.dma_start(out=o, in_=i)), (sem, 16))
        self._update(ev, r, w)
        return ev

    def barrier(self):
        evs = []
        for e in ENGS:
            if self.cnt[e] > 0:
                sem, val = self._esem(e, self.cnt[e])
                clock = {f: 0 for f in ENGS}; clock[e] = self.cnt[e]
                evs.append(Ev(e, self.cnt[e], sem, val, clock))
        dm = [d for d in self.all_dma]
        for e in ENGS:
            waits = []
            kn = self.know[e]
            for d in evs:
                if d.eng == e:
                    if e not in self.sync_same or kn[e] >= d.seq:
                        continue
                elif kn[d.eng] >= d.seq:
                    continue
                waits.append((d.sem, d.val)); kn[d.eng] = max(kn[d.eng], d.seq)
            for q in self.dsems:
                for j in range(NDSEM):
                    d = self.dlast[q][j]
                    if d is not None and d.did not in self.dknow[e]:
                        self.dknow[e].add(d.did)
                        waits.append((d.sem, d.val))
            if waits:
                self._issue(e, waits, None, None)
        self.all_dma = []
        self.state = {}

    def finish(self):
        self.barrier()
        while self.scopes:
            self.scopes.pop().close()
        self.es.close()


T = 2304; NCTX = 256; NLAT = 2048; D = 1024; NB = 8
HP = 2368
EPS = 1e-6
IN_COLS = 10112
RW0, CV0, AT0, GT0 = 0, 3456, 5504, 7040
TILES = [(0, 256, 1)] + [(256 + i * 512, 512, 0) for i in range(4)]
TILES256 = [(0, 256, 1)] + [(256 + i * 256, 256, 0) for i in range(8)]
COLT = [(i * 512, 512) for i in range(4)] + [(2048, 320)]


def col(t):
    return t + 16 if t < 256 else t + 48


def fm(ap):
    return ap.rearrange("(b p) t -> p b t", p=128)


class K:
    def __init__(self, p):
        self.p = p

    def mm(self, out, lhsT, rhs, start, stop, r, w):
        self.p.op('tensor', lambda e: e.matmul(out, lhsT, rhs, start=start, stop=stop), r=r, w=w)

    def tr(self, out, in_, ident, r, w):
        self.p.op('tensor', lambda e: e.transpose(out, in_, ident), r=r, w=w)

    def act(self, out, in_, func, r, w, scale=1.0, bias=0.0, eng='scalar'):
        self.p.op('scalar', lambda e: e.activation(out=out, in_=in_, func=func, bias=bias, scale=scale), r=r, w=w)

    def tt(self, eng, out, in0, in1, op, r, w):
        self.p.op(eng, lambda e: e.tensor_tensor(out=out, in0=in0, in1=in1, op=op), r=r, w=w)

    def ts(self, eng, out, in0, s1, s2, op0, op1, r, w):
        if op1 is None:
            self.p.op(eng, lambda e: e.tensor_scalar(out=out, in0=in0, scalar1=s1, scalar2=None, op0=op0), r=r, w=w)
        else:
            self.p.op(eng, lambda e: e.tensor_scalar(out=out, in0=in0, scalar1=s1, scalar2=s2, op0=op0, op1=op1), r=r, w=w)

    def stt(self, eng, out, in0, scalar, in1, op0, op1, r, w):
        self.p.op(eng, lambda e: e.scalar_tensor_tensor(out=out, in0=in0, scalar=scalar, in1=in1, op0=op0, op1=op1), r=r, w=w)

    def cp(self, eng, out, in_, r, w):
        if eng == 'scalar':
            self.p.op(eng, lambda e: e.copy(out=out, in_=in_), r=r, w=w)
        else:
            self.p.op(eng, lambda e: e.tensor_copy(out=out, in_=in_), r=r, w=w)

    def recip(self, out, in_, r, w):
        self.p.op('vector', lambda e: e.reciprocal(out=out, in_=in_), r=r, w=w)

    def memset(self, eng, ap, val, w):
        self.p.op(eng, lambda e: e.memset(ap, val), r=[], w=w)

    def dma(self, out, in_, r=(), w=(), q='sync'):
        self.p.dma(q, out, in_, r=r, w=w)


def load_w_bf16(k, w_dram, dst, key, stg, ceng='gpsimd', col0=0, ncols=None):
    Kd, N = w_dram.shape
    if ncols is None:
        ncols = N
    KC = Kd // 128
    src = w_dram.rearrange("(k p) n -> p k n", p=128)
    nn = min(ncols, 512)
    kk = min(KC, 4096 // nn)
    cnt = getattr(k, '_stgcnt', 0)
    for k0 in range(0, KC, kk):
        for n0 in range(0, ncols, nn):
            n1 = min(nn, ncols - n0)
            st = stg[cnt % 2]
            sk = ('stg', cnt % 2)
            sv = st[:, 0:kk * n1].rearrange("p (k n) -> p k n", n=n1)
            k.dma(sv, src[:, k0:k0 + kk, col0 + n0:col0 + n0 + n1], w=[sk])
            k.cp(ceng, dst[:, k0:k0 + kk, n0:n0 + n1], sv, r=[sk], w=[key])
            cnt += 1
    k._stgcnt = cnt


def prenorm(k, G, xk, x, n, gi, si, s, hk, h, sq, rstd, psn, tag):
    mv = G['mv']
    k.act(sq[:, :, :n], x[:, :, :n], AF.Square, r=[xk], w=['sq' + tag])
    for b in range(8):
        k.mm(psn[:, :n], G['ones_f'][:], sq[:, b, :n], b == 0, b == 7, r=['sq' + tag, 'const'], w=['psn' + tag])
    k.act(rstd[:, :n], psn[:, :n], AF.Sqrt, r=['psn' + tag], w=['rstd' + tag], scale=1.0 / 1024, bias=EPS)
    k.recip(rstd[:, :n], rstd[:, :n], r=['rstd' + tag], w=['rstd' + tag])
    for b in range(8):
        k.tt('vector', sq[:, b, :n], x[:, b, :n], rstd[:, :n], ALU.mult, r=[xk, 'rstd' + tag], w=['sq' + tag])
        k.act(h[:, b, :n], sq[:, b, :n], AF.Identity, r=['sq' + tag, 'mv'], w=[hk],
              scale=mv[:, gi, b, s:s + 1], bias=mv[:, si, b, s:s + 1])


def postnorm_res(k, G, mk_, m, n, gatei, s, xk, x, sq, rstd, psn, tag):
    mv = G['mv']
    k.act(sq[:, :, :n], m[:, :, :n], AF.Square, r=[mk_], w=['sq' + tag])
    for b in range(8):
        k.mm(psn[:, :n], G['ones_f'][:], sq[:, b, :n], b == 0, b == 7, r=['sq' + tag, 'const'], w=['psn' + tag])
    k.act(rstd[:, :n], psn[:, :n], AF.Sqrt, r=['psn' + tag], w=['rstd' + tag], scale=1.0 / 1024, bias=EPS)
    k.recip(rstd[:, :n], rstd[:, :n], r=['rstd' + tag], w=['rstd' + tag])
    for b in range(8):
        k.tt('vector', sq[:, b, :n], m[:, b, :n], rstd[:, :n], ALU.mult, r=[mk_, 'rstd' + tag], w=['sq' + tag])
        k.stt('vector', x[:, b, :n], sq[:, b, :n], mv[:, gatei, b, s:s + 1], x[:, b, :n], ALU.mult, ALU.add,
              r=['sq' + tag, 'mv', xk], w=[xk])


def stage_mod(k, G, W):
    p = k.p
    p.push()
    sc = p.sb('sc', [128, 8, 2], F32)
    modraw = p.sb('modraw', [128, 48, 2], F32)
    wst = [p.sb('wmst', [128, 8, 768], F32) for _ in range(2)]
    psm = p.ps('psm', [128, 48, 2], F32)
    cT = p.sb('cT', [128, 8, 2], F32)
    bm = p.sb('bm', [128, 48], F32)
    nrm = p.sb('nrm', [128, 4, 8], F32)
    k.dma(cT[:], G['cT'], w=['cT'])
    k.dma(bm[:], W['b_modT'], w=['bm'])
    k.dma(nrm[:], W['nrm'], w=['nrm'])
    k.act(sc[:], cT[:], AF.Silu, r=['cT'], w=['sc'])
    src = W['w_mod'].rearrange("(k p) n -> p k n", p=128)
    for pn in range(8):
        st = wst[pn % 2]; sk = ('wmst', pn % 2)
        k.dma(st[:], src[:, :, pn * 768:(pn + 1) * 768], w=[sk])
        for mi in range(6):
            m = pn * 6 + mi
            for kc in range(8):
                k.mm(psm[:, m, :], st[:, kc, mi * 128:(mi + 1) * 128], sc[:, kc, :], kc == 0, kc == 7,
                     r=[sk, 'sc'], w=['psm'])
    for s in range(2):
        k.tt('vector', modraw[:, :, s], psm[:, :, s], bm[:], ALU.add, r=['psm', 'bm'], w=['modraw'])
    mv = G['mv']
    for s in range(2):
        k.stt('vector', mv[:, 0, :, s], modraw[:, 8:16, s], 1.0, nrm[:, 0, :], ALU.add, ALU.mult, r=['modraw', 'nrm'], w=['mv'])
        k.cp('vector', mv[:, 1, :, s], modraw[:, 0:8, s], r=['modraw'], w=['mv'])
        k.tt('vector', mv[:, 2, :, s], modraw[:, 16:24, s], nrm[:, 1, :], ALU.mult, r=['modraw', 'nrm'], w=['mv'])
        k.stt('vector', mv[:, 3, :, s], modraw[:, 32:40, s], 1.0, nrm[:, 2, :], ALU.add, ALU.mult, r=['modraw', 'nrm'], w=['mv'])
        k.cp('vector', mv[:, 4, :, s], modraw[:, 24:32, s], r=['modraw'], w=['mv'])
        k.tt('vector', mv[:, 5, :, s], modraw[:, 40:48, s], nrm[:, 3, :], ALU.mult, r=['modraw', 'nrm'], w=['mv'])
    if 'dbg_mv' in G:
        k.dma(G['dbg_mv'], mv[:], r=['mv'])
    p.pop()
    p.state['mv'] = [None, {}, []]


def stage_prenorm(k, G, x_dram, hT_dram):
    p = k.p
    p.push()
    xs = [p.sb('x', [128, 8, 512], F32) for _ in range(2)]
    hs = [p.sb('h', [128, 8, 512], BF16) for _ in range(2)]
    sq = p.sb('sq', [128, 8, 512], F32)
    rstd = p.sb('rstd', [128, 512], F32)
    psn = [p.ps('psn', [128, 512], F32) for _ in range(2)]
    zt = p.sb('zt', [128, 8, 16], BF16)
    k.memset('vector', zt[:], 0.0, w=['zt'])
    hv = fm(hT_dram)
    for c0 in (0, 272, 288, 2352):
        k.dma(hv[:, :, c0:c0 + 16], zt[:], r=['zt'])
    xv = fm(x_dram)
    for i, (t0, n, s) in enumerate(TILES):
        x = xs[i % 2]; h = hs[i % 2]
        k.dma(x[:, :, :n], xv[:, :, t0:t0 + n], w=[('x', i % 2)])
        prenorm(k, G, ('x', i % 2), x, n, 0, 1, s, ('h', i % 2), h, sq, rstd, psn[i % 2], str(i % 2) if False else '')
        k.dma(hv[:, :, col(t0):col(t0) + n], h[:, :, :n], r=[('h', i % 2)])
    p.pop()


def stage_mlp(k, G, W, x_dram):
    p = k.p
    p.push()
    w1 = p.sb('w1', [128, 8, 4096], BF16)
    w2 = p.sb('w2', [128, 32, 1024], BF16)
    p.push()
    stg = [p.sb('stg', [128, 4096], F32) for _ in range(2)]
    load_w_bf16(k, W['w_ff1'], w1, 'w1', stg, 'gpsimd')
    load_w_bf16(k, W['w_ff2'], w2, 'w2', stg, 'gpsimd')
    p.pop()
    xs = [p.sb('x', [128, 8, 256], F32) for _ in range(2)]
    h2 = p.sb('h2', [128, 8, 256], BF16)
    f1 = p.sb('f1', [128, 32, 256], BF16)
    m = p.sb('m', [128, 8, 256], F32)
    sq = p.sb('sq', [128, 8, 256], F32)
    rstd = p.sb('rstd', [128, 256], F32)
    tmp = [p.sb('tmp', [128, 256], F32) for _ in range(2)]
    psn = p.ps('psn', [128, 512], F32)
    psf = [p.ps('psf', [128, 512], F32) for _ in range(3)]
    pso = [p.ps('pso', [128, 512], F32) for _ in range(2)]
    xv = fm(x_dram)
    for i, (t0, n, s) in enumerate(TILES256):
        x = xs[i % 2]; xk = ('x', i % 2)
        k.dma(x[:, :, :n], xv[:, :, t0:t0 + n], w=[xk])
        prenorm(k, G, xk, x, n, 3, 4, s, 'h2', h2, sq, rstd, psn, '')
        for fb in range(32):
            ps = psf[fb % 3]; pk = ('psf', fb % 3)
            for kc in range(8):
                k.mm(ps[:, :n], w1[:, kc, fb * 128:(fb + 1) * 128], h2[:, kc, :n], kc == 0, kc == 7, r=['w1', 'h2'], w=[pk])
            tm = tmp[fb % 2]; tk = ('tmp', fb % 2)
            k.act(tm[:, :n], ps[:, :n], AF.Relu, r=[pk], w=[tk])
            k.tt('vector' if fb % 2 else 'gpsimd', f1[:, fb, :n], tm[:, :n], tm[:, :n], ALU.mult, r=[tk], w=[('f1', fb)])
        for db in range(8):
            ps = pso[db % 2]; pk = ('pso', db % 2)
            for fk in range(32):
                k.mm(ps[:, :n], w2[:, fk, db * 128:(db + 1) * 128], f1[:, fk, :n], fk == 0, fk == 31, r=['w2', ('f1', fk)], w=[pk])
            k.cp('scalar', m[:, db, :n], ps[:, :n], r=[pk], w=['m'])
        postnorm_res(k, G, 'm', m, n, 5, s, xk, x, sq, rstd, psn, '')
        k.dma(xv[:, :, t0:t0 + n], x[:, :, :n], r=[xk])
    p.pop()


def load_hT(k, hT_dram, hsb, key='hT'):
    hv = fm(hT_dram)
    for b0 in range(0, 8, 2):
        k.dma(hsb[:, b0:b0 + 2, :], hv[:, b0:b0 + 2, :], w=[(key, b0)])
    return [(key, (b // 2) * 2) for b in range(8)]


def stage_conv(k, G, W, hT_dram, ocvT):
    p = k.p
    p.push()
    hsb = p.sb('hsb', [128, 8, HP], BF16)
    hk = load_hT(k, hT_dram, hsb)
    wcv = p.sb('wcv', [128, 8, 2048], BF16)
    p.push()
    stg = [p.sb('stg', [128, 4096], F32) for _ in range(2)]
    load_w_bf16(k, W['w_in'], wcv, 'wcv', stg, 'gpsimd', col0=CV0, ncols=2048)
    p.pop()
    glu = p.sb('glu', [128, 8, HP], BF16)
    cvw = p.sb('cvw', [128, 8, 31], F32)
    cvv = p.sb('cvv', [128, 3, 8], F32)
    identb = p.sb('identb', [128, 128], BF16)
    k.dma(cvw[:], W['cv_w'], w=['cvw'])
    k.dma(cvv[:], W['cv_v'], w=['cvv'])
    k.cp('vector', identb[:], G['ident_f'][:], r=['const'], w=['identb'])
    sg = [p.sb('sg', [128, 512], F32) for _ in range(2)]
    psa = [p.ps('psa', [128, 512], F32) for _ in range(2)]
    psb = [p.ps('psb', [128, 512], F32) for _ in range(2)]
    it = 0
    for (c0, n) in COLT:
        for b in range(8):
            pa = psa[it % 2]; pb = psb[it % 2]; s_ = sg[it % 2]
            for kc in range(8):
                k.mm(pa[:, :n], wcv[:, kc, b * 128:(b + 1) * 128], hsb[:, kc, c0:c0 + n], kc == 0, kc == 7,
                     r=['wcv', hk[kc]], w=[('psa', it % 2)])
            for kc in range(8):
                k.mm(pb[:, :n], wcv[:, kc, 1024 + b * 128:1024 + (b + 1) * 128], hsb[:, kc, c0:c0 + n], kc == 0, kc == 7,
                     r=['wcv', hk[kc]], w=[('psb', it % 2)])
            k.act(s_[:, :n], pb[:, :n], AF.Sigmoid, r=[('psb', it % 2)], w=[('sg', it % 2)])
            k.tt('vector', glu[:, b, c0:c0 + n], pa[:, :n], s_[:, :n], ALU.mult, r=[('psa', it % 2), ('sg', it % 2)], w=[('glu', b)])
            it += 1
    dg = [p.sb('dg', [128, 31, 128], BF16) for _ in range(2)]
    u = p.sb('u', [128, 8, 512], F32)
    sq = p.sb('sq', [128, 8, 512], F32)
    mean = p.sb('mean', [128, 512], F32)
    msq = p.sb('msq', [128, 512], F32)
    rstd = p.sb('rstd', [128, 512], F32)
    ob = [p.sb('ob', [128, 8, 512], BF16) for _ in range(2)]
    psu = [p.ps('psu', [128, 512], F32) for _ in range(2)]
    psm = p.ps('psm', [128, 512], F32)
    pss = p.ps('pss', [128, 512], F32)
    ov = fm(ocvT)
    it = 0
    for ti, (t0, n, s) in enumerate(TILES):
        cc = col(t0)
        for b in range(8):
            d = dg[it % 2]; dk = ('dg', it % 2); pu = psu[it % 2]; pk = ('psu', it % 2)
            for tap in range(31):
                k.ts('vector' if tap % 2 else 'gpsimd', d[:, tap, :], identb[:], cvw[:, b, tap:tap + 1], None, ALU.mult, None,
                     r=['identb', 'cvw'], w=[(dk, tap)])
            for tap in range(31):
                k.mm(pu[:, :n], d[:, tap, :], glu[:, b, cc + tap - 15:cc + tap - 15 + n], tap == 0, tap == 30,
                     r=[(dk, tap), ('glu', b)], w=[pk])
            k.act(u[:, b, :n], pu[:, :n], AF.Identity, r=[pk, 'cvv'], w=[('u', b)], bias=cvv[:, 0, b:b + 1])
            it += 1
        for b in range(8):
            k.mm(psm[:, :n], G['ones_f'][:], u[:, b, :n], b == 0, b == 7, r=[('u', b), 'const'], w=['psm'])
        k.act(sq[:, :, :n], u[:, :, :n], AF.Square, r=[('u', b) for b in range(8)], w=['sq'])
        for b in range(8):
            k.mm(pss[:, :n], G['ones_f'][:], sq[:, b, :n], b == 0, b == 7, r=['sq', 'const'], w=['pss'])
        k.act(mean[:, :n], psm[:, :n], AF.Identity, r=['psm'], w=['mean'], scale=1.0 / 1024)
        k.tt('vector', msq[:, :n], mean[:, :n], mean[:, :n], ALU.mult, r=['mean'], w=['msq'])
        k.stt('vector', msq[:, :n], pss[:, :n], 1.0 / 1024, msq[:, :n], ALU.mult, ALU.subtract, r=['pss', 'msq'], w=['msq'])
        k.act(rstd[:, :n], msq[:, :n], AF.Sqrt, r=['msq'], w=['rstd'], bias=EPS)
        k.recip(rstd[:, :n], rstd[:, :n], r=['rstd'], w=['rstd'])
        o = ob[ti % 2]; ok_ = ('ob', ti % 2)
        for b in range(8):
            k.tt('vector', sq[:, b, :n], u[:, b, :n], mean[:, :n], ALU.subtract, r=[('u', b), 'mean'], w=['sq'])
            k.tt('gpsimd', sq[:, b, :n], sq[:, b, :n], rstd[:, :n], ALU.mult, r=['sq', 'rstd'], w=['sq'])
            k.act(o[:, b, :n], sq[:, b, :n], AF.Silu, r=['sq', 'cvv'], w=[ok_], scale=cvv[:, 1, b:b + 1], bias=cvv[:, 2, b:b + 1])
        k.dma(ov[:, :, t0:t0 + n], o[:, :, :n], r=[ok_])
    p.pop()


def stage_attn(k, G, W, hT_dram, oatT):
    p = k.p
    p.push()
    hsb = p.sb('hsb', [128, 8, HP], BF16)
    hk = load_hT(k, hT_dram, hsb)
    wat = p.sb('wat', [128, 8, 1536], BF16)
    p.push()
    stg = [p.sb('stg', [128, 4096], F32) for _ in range(2)]
    load_w_bf16(k, W['w_in'], wat, 'wat', stg, 'gpsimd', col0=AT0, ncols=1536)
    p.pop()
    QT = p.sb('QT', [128, 8, T], BF16)
    KT = p.sb('KT', [128, 2, T], BF16)
    V = p.sb('V', [128, 18, 256], BF16)
    cosT = p.sb('cosT', [128, 2048], F32)
    sinT = p.sb('sinT', [128, 2048], F32)
    atg = p.sb('atg', [128, 2], F32)
    onesb = p.sb('onesb', [128, 128], BF16)
    k.dma(cosT[:], G['cosT'], w=['cos'])
    k.dma(sinT[:], G['sinT'], w=['sin'])
    k.dma(atg[:], W['at_g'], w=['atg'])
    k.cp('vector', onesb[:], G['ones_f'][:], r=['const'], w=['onesb'])
    qraw = p.sb('qraw', [128, 512], F32)
    sq = p.sb('sq', [128, 512], F32)
    rstd = p.sb('rstd', [128, 512], F32)
    qn = p.sb('qn', [128, 512], F32)
    t1 = p.sb('t1', [128, 512], F32)
    t2 = p.sb('t2', [128, 512], F32)
    psq = [p.ps('psq', [128, 512], F32) for _ in range(2)]
    psn = p.ps('psn', [128, 512], F32)
    psr = p.ps('psr', [128, 512], F32)
    it = 0
    for hh in range(10):
        for (t0, n, s) in TILES:
            pq = psq[it % 2]; pk = ('psq', it % 2); it += 1
            cc = col(t0)
            for kc in range(8):
                k.mm(pq[:, :n], wat[:, kc, hh * 128:(hh + 1) * 128], hsb[:, kc, cc:cc + n], kc == 0, kc == 7,
                     r=['wat', hk[kc]], w=[pk])
            k.cp('scalar', qraw[:, :n], pq[:, :n], r=[pk], w=['qraw'])
            k.act(sq[:, :n], pq[:, :n], AF.Square, r=[pk], w=['sq'])
            k.mm(psn[:, :n], G['ones_f'][:], sq[:, :n], True, True, r=['sq', 'const'], w=['psn'])
            k.act(rstd[:, :n], psn[:, :n], AF.Sqrt, r=['psn'], w=['rstd'], scale=1.0 / 128, bias=EPS)
            k.recip(rstd[:, :n], rstd[:, :n], r=['rstd'], w=['rstd'])
            gcol = atg[:, 0:1] if hh < 8 else atg[:, 1:2]
            dst = QT[:, hh, t0:t0 + n] if hh < 8 else KT[:, hh - 8, t0:t0 + n]
            dk = ('QT', hh)
            if s == 1:
                k.stt('vector', dst, qraw[:, :n], gcol, rstd[:, :n], ALU.mult, ALU.mult, r=['qraw', 'rstd', 'atg'], w=[dk])
            else:
                l0 = t0 - 256
                k.stt('vector', qn[:, :n], qraw[:, :n], gcol, rstd[:, :n], ALU.mult, ALU.mult, r=['qraw', 'rstd', 'atg'], w=['qn'])
                k.mm(psr[:, :n], G['rsw'][:], qn[:, :n], True, True, r=['qn', 'const'], w=['psr'])
                k.tt('gpsimd', t1[:, :n], qn[:, :n], cosT[:, l0:l0 + n], ALU.mult, r=['qn', 'cos'], w=['t1'])
                k.tt('vector', t2[:, :n], psr[:, :n], sinT[:, l0:l0 + n], ALU.mult, r=['psr', 'sin'], w=['t2'])
                k.tt('vector', dst, t1[:, :n], t2[:, :n], ALU.add, r=['t1', 't2'], w=[dk])
    for j in range(18):
        pq = psq[it % 2]; pk = ('psq', it % 2); it += 1
        cc = col(j * 128)
        for kc in range(8):
            k.mm(pq[:, :256], hsb[:, kc, cc:cc + 128], wat[:, kc, 1280:1536], kc == 0, kc == 7, r=['wat', hk[kc]], w=[pk])
        k.cp('scalar', V[:, j, :], pq[:, :256], r=[pk], w=[('V', j)])
    pT = [p.sb('pT', [128, 512], BF16) for _ in range(3)]
    rden = p.sb('rden', [128, 512], F32)
    oo = [p.sb('oo', [128, 512], BF16) for _ in range(2)]
    pss = [p.ps('pss', [128, 512], F32) for _ in range(2)]
    pso = p.ps('pso', [128, 512], F32)
    psd = p.ps('psd', [128, 512], F32)
    ov = oatT
    sc = 128.0 ** -0.5
    it = 0; io = 0
    for h in range(8):
        kv = h // 4
        for (t0, n, s) in TILES:
            nk = 2 if s == 1 else 18
            for j in range(nk):
                ps = pss[it % 2]; pk = ('pss', it % 2); pt = pT[it % 3]; ptk = ('pT', it % 3); it += 1
                k.mm(ps[:, :n], KT[:, kv, j * 128:(j + 1) * 128], QT[:, h, t0:t0 + n], True, True,
                     r=[('QT', 8 + kv), ('QT', h)], w=[pk])
                k.act(pt[:, :n], ps[:, :n], AF.Exp, r=[pk], w=[ptk], scale=sc)
                k.mm(pso[:, :n], V[:, j, kv * 128:(kv + 1) * 128], pt[:, :n], j == 0, j == nk - 1, r=[('V', j), ptk], w=['pso'])
                k.mm(psd[:, :n], onesb[:], pt[:, :n], j == 0, j == nk - 1, r=['onesb', ptk], w=['psd'])
            o = oo[io % 2]; ok_ = ('oo', io % 2); io += 1
            k.recip(rden[:, :n], psd[:, :n], r=['psd'], w=['rden'])
            k.tt('vector', o[:, :n], pso[:, :n], rden[:, :n], ALU.mult, r=['pso', 'rden'], w=[ok_])
            k.dma(ov[h * 128:(h + 1) * 128, t0:t0 + n], o[:, :n], r=[ok_])
    p.pop()


def stage_merge(k, G, W, hT_dram, orwT, ocvT, oatT, x_src, x_dst):
    p = k.p
    p.push()
    hsb = p.sb('hsb', [128, 8, HP], BF16)
    hk = load_hT(k, hT_dram, hsb)
    wg = p.sb('wg', [128, 8, 3072], BF16)
    wb = p.sb('wb', [128, 24, 1024], BF16)
    wo = p.sb('wo', [128, 8, 1024], BF16)
    p.push()
    stg = [p.sb('stg', [128, 4096], F32) for _ in range(2)]
    load_w_bf16(k, W['w_in'], wg, 'wg', stg, 'gpsimd', col0=GT0, ncols=3072)
    load_w_bf16(k, W['w_branch'], wb, 'wb', stg, 'gpsimd')
    load_w_bf16(k, W['w_out'], wo, 'wo', stg, 'gpsimd')
    p.pop()
    obr = [p.sb('obr', [128, 8, 256], BF16) for _ in range(3)]
    sgt = [p.sb('sgt', [128, 256], F32) for _ in range(2)]
    tmp = p.sb('tmp', [128, 256], F32)
    macc = p.sb('macc', [128, 256], F32)
    mrg = p.sb('mrg', [128, 8, 256], BF16)
    mo = p.sb('mo', [128, 8, 256], F32)
    x = p.sb('x', [128, 8, 256], F32)
    sq = p.sb('sq', [128, 8, 256], F32)
    rstd = p.sb('rstd', [128, 256], F32)
    psg = [p.ps('psg', [128, 512], F32) for _ in range(2)]
    psp = [p.ps('psp', [128, 512], F32) for _ in range(2)]
    pso = [p.ps('pso', [128, 512], F32) for _ in range(2)]
    psn = p.ps('psn', [128, 512], F32)
    srcs = [fm(orwT), fm(ocvT), fm(oatT)]
    xs = fm(x_src); xd = fm(x_dst)
    it = 0
    for (t0, n, s) in TILES256:
        cc = col(t0)
        for br in range(3):
            k.dma(obr[br][:, :, :n], srcs[br][:, :, t0:t0 + n], w=[('obr', br)])
        k.dma(x[:, :, :n], xs[:, :, t0:t0 + n], w=['x'])
        for db in range(8):
            for br in range(3):
                pg = psg[it % 2]; pgk = ('psg', it % 2); pp = psp[it % 2]; ppk = ('psp', it % 2); sg = sgt[it % 2]; sgk = ('sgt', it % 2)
                it += 1
                for kc in range(8):
                    k.mm(pg[:, :n], wg[:, kc, br * 1024 + db * 128: br * 1024 + (db + 1) * 128], hsb[:, kc, cc:cc + n], kc == 0, kc == 7,
                         r=['wg', hk[kc]], w=[pgk])
                for kc in range(8):
                    k.mm(pp[:, :n], wb[:, br * 8 + kc, db * 128:(db + 1) * 128], obr[br][:, kc, :n], kc == 0, kc == 7,
                         r=['wb', ('obr', br)], w=[ppk])
                k.act(sg[:, :n], pg[:, :n], AF.Sigmoid, r=[pgk], w=[sgk])
                if br == 0:
                    k.tt('vector', macc[:, :n], pp[:, :n], sg[:, :n], ALU.mult, r=[ppk, sgk], w=['macc'])
                else:
                    k.tt('vector', tmp[:, :n], pp[:, :n], sg[:, :n], ALU.mult, r=[ppk, sgk], w=['tmp'])
                    if br == 1:
                        k.tt('gpsimd', macc[:, :n], macc[:, :n], tmp[:, :n], ALU.add, r=['tmp', 'macc'], w=['macc'])
                    else:
                        k.tt('gpsimd', mrg[:, db, :n], macc[:, :n], tmp[:, :n], ALU.add, r=['tmp', 'macc'], w=[('mrg', db)])
        for ob_ in range(8):
            po = pso[ob_ % 2]; pok = ('pso', ob_ % 2)
            for dc in range(8):
                k.mm(po[:, :n], wo[:, dc, ob_ * 128:(ob_ + 1) * 128], mrg[:, dc, :n], dc == 0, dc == 7, r=['wo', ('mrg', dc)], w=[pok])
            k.cp('scalar', mo[:, ob_, :n], po[:, :n], r=[pok], w=['mo'])
        if 'dbg_mix' in G:
            k.dma(fm(G['dbg_mix'])[:, :, t0:t0 + n], mo[:, :, :n], r=['mo'])
        postnorm_res(k, G, 'mo', mo, n, 2, s, 'x', x, sq, rstd, psn, '')
        k.dma(xd[:, :, t0:t0 + n], x[:, :, :n], r=['x'])
    p.pop()


CDEC = 0.6065306597126334
RW_LIMIT = 18
RW_STOP = 9
RW_H = 99
CH_F = list(range(18))
CH_B = [1, 0] + list(range(17, 1, -1))
RW_GN_EPS = 64e-5


def stage_rwkv(k, G, W, hT_dram, yf_dram, orwT):
    p = k.p
    p.push()
    wrw = p.sb('wrw', [128, 8, 3456], BF16)
    g2b = p.sb('g2b', [128, 1024], BF16)
    p.push()
    stg = [p.sb('stg', [128, 4096], F32) for _ in range(2)]
    load_w_bf16(k, W['w_in'], wrw, 'wrw', stg, 'gpsimd', col0=RW0, ncols=3456)
    k.dma(stg[0][:, 0:1024], W['rw_g2'], w=[('stg', 0)])
    k.cp('gpsimd', g2b[:], stg[0][:, 0:1024], r=[('stg', 0)], w=['g2b'])
    p.pop()
    C = 'rwc'
    tri = p.sb('tri', [128, 5, 128], F32)
    k.dma(tri[:], G['tri'], w=[C])
    identb = p.sb('identb', [128, 128], BF16)
    k.cp('vector', identb[:], G['ident_f'][:], r=['const'], w=[C])
    mjt = []
    for d in range(2):
        m = p.sb('mjt', [128, 2, 2, 128], BF16)
        for hd in range(2):
            k.cp('vector', m[:, hd, 0, :], tri[:, 2 + d, :], r=[C], w=[C])
            k.cp('vector', m[:, hd, 1, :], tri[:, d, :], r=[C], w=[C])
        mjt.append(m)
    mdjt = []; mnjt = []; mdtj = []
    nmask = p.sb('nmask', [128, 128], F32)
    k.ts('vector', nmask[:], tri[:, 4, :], -1.0, 1.0, ALU.mult, ALU.add, r=[C], w=[C])
    for d in range(2):
        a_ = p.sb('mdjt', [128, 4, 128], BF16); b_ = p.sb('mnjt', [128, 4, 128], BF16); c_ = p.sb('mdtj', [128, 4, 128], BF16)
        for i in range(4):
            k.tt('vector', a_[:, i, :], tri[:, 2 + d, :], tri[:, 4, :], ALU.mult, r=[C], w=[C])
            k.tt('vector', b_[:, i, :], tri[:, 2 + d, :], nmask[:], ALU.mult, r=[C], w=[C])
            k.tt('vector', c_[:, i, :], tri[:, 3 - d, :], tri[:, 4, :], ALU.mult, r=[C], w=[C])
        mdjt.append(a_); mnjt.append(b_); mdtj.append(c_)
    w2s = p.sb('w2s', [128, 1024], F32); a2s = p.sb('a2s', [128, 1024], F32)
    k.dma(w2s[:], W['rw_w2'], w=[C]); k.dma(a2s[:], W['rw_a2'], w=[C])
    brow = p.sb('brow', [128, 1024], F32)
    k.memset('vector', brow[:], 0.0, w=[C])
    k.dma(brow[0:1, :], W['rw_w0'][0:1, :], w=[C]); k.dma(brow[64:65, :], W['rw_w0'][1:2, :], w=[C])
    brow2 = p.sb('brow2', [128, 1024], F32)
    k.memset('vector', brow2[:], 0.0, w=[C])
    k.dma(brow2[0:1, :], W['rw_a0'][0:1, :], w=[C]); k.dma(brow2[64:65, :], W['rw_a0'][1:2, :], w=[C])
    rwv = p.sb('rwv', [128, 2, 1024], F32)
    k.dma(rwv[:], W['rwv'][:, 0:2, :], w=[C])
    rwvb = p.sb('rwvb', [128, 3, 1024], BF16)
    mu = p.sb('mu', [128, 27], F32)
    k.dma(mu[:], W['rw_mu'], w=[C])
    ones_f = G['ones_f']; ident_f = G['ident_f']
    hcs = [p.sb('hcol', [128, 8, 130], BF16)] * 2
    zsb = [p.sb('zs', [128, 130], F32) for _ in range(2)]
    zt1 = [p.sb('zt1', [128, 128], F32) for _ in range(2)]
    zl4 = [p.sb('zl4', [128, 4, 128], F32) for _ in range(2)]
    tw = p.sb('tw', [128, 128], F32); ta = p.sb('ta', [128, 128], F32); tg = p.sb('tg', [128, 128], BF16)
    WT = {n: p.sb(n, [128, 1024], F32) for n in ['tR', 'tK', 'tKK', 'tSW', 'tB', 'tmp', 'tA', 'tY']}
    WT['tG'] = p.sb('tG', [128, 1024], BF16)
    for i_, n_ in enumerate(['tR', 'tK', 'tKK']):
        k.dma(WT[n_][:], W['rwv'][:, 2 + i_, :], w=[n_])
        k.cp('vector', rwvb[:, i_, :], WT[n_][:], r=[n_], w=[C])
    BT_ = {n: p.sb(n, [128, 1024], BF16) for n in ['RTm', 'BEm', 'KTm', 'BTm', 'KHm', 'BHm', 'Vb']}
    XT = p.sb('XT', [64, 16, 4, 128], BF16)
    SB1 = p.sb('SB1', [128, 2, 2, 256], BF16); SB2 = p.sb('SB2', [128, 2, 2, 256], BF16)
    AdT = p.sb('AdT', [128, 4, 128], F32); NT = p.sb('NT', [128, 4, 128], F32)
    Pf = [p.sb('Pf', [128, 4, 128], F32) for _ in range(2)]
    PTf = [AdT, p.sb('PTf', [128, 4, 128], F32)]
    IP = p.sb('IP', [128, 4, 128], F32)
    FT = [p.sb('FT', [128, 4, 128], F32) for _ in range(2)]
    W32 = p.sb('W32', [128, 4, 64], F32); U32 = p.sb('U32', [128, 4, 64], F32); V32 = p.sb('V32', [128, 4, 64], F32)
    Xb = p.sb('Xb', [128, 4, 64], BF16)
    Tst = p.sb('Tst', [64, 16, 64], F32); Tb = p.sb('Tb', [64, 16, 64], BF16)
    PCx = p.sb('PCx', [64, 16, 64], F32)
    st16 = [p.sb('st16', [128, 16], F32) for _ in range(4)]
    Bk = [p.ps('bk', [128, 512], F32) for _ in range(7)]
    Bb = p.ps('bkb', [128, 1024], BF16)
    def bk(i): return ('B', i)
    wide = [Bk[3], Bk[4]]

    def v3(t, a, b_):
        return t[:, 0:a * b_].rearrange("p (a b) -> p a b", b=b_)

    hv = fm(hT_dram)
    cnt = {'z': 0}

    def prep(c, d, bwd):
        cc = col(128 * c)
        i = cnt['z']; cnt['z'] += 1
        hc = hcs[0]; hck = ('hcol', 0)
        k.dma(hc[:], hv[:, :, cc - 1:cc + 129], w=[hck])
        for blk in range(27):
            if blk == 26 and not bwd:
                continue
            j = blk % 2
            ps = Bk[0][:, j * 130:(j + 1) * 130]
            for kc in range(8):
                k.mm(ps, wrw[:, kc, blk * 128:(blk + 1) * 128], hc[:, kc, :], kc == 0, kc == 7, r=['wrw', hck], w=[bk(0)])
            zs = zsb[j]; zk = ('zs', j); t1 = zt1[j]; tk = ('zt1', j)
            k.cp('scalar', zs[:], ps, r=[bk(0)], w=[zk])
            k.tt('gpsimd', t1[:], zs[:, 0:128], zs[:, 2:130], ALU.add, r=[zk], w=[tk])
            k.stt('vector', t1[:], t1[:], 0.5, zs[:, 1:129], ALU.mult, ALU.subtract, r=[tk, zk], w=[tk])
            if blk < 24:
                g4 = blk // 4; zl = zl4[g4 % 2]; zlk = ('zl4', g4 % 2)
                k.stt('vector', zl[:, blk % 4, :], t1[:], mu[:, blk:blk + 1], zs[:, 1:129], ALU.mult, ALU.add, r=[tk, zk, C], w=[zlk])
                if blk % 4 == 3:
                    pst = v3(Bk[1], 4, 128)
                    for q in range(4):
                        k.tr(pst[:, q, :], zl[:, q, :], ident_f[:], r=[zlk, 'const'], w=[bk(1)])
                    kind = blk // 8; half = (blk % 8) // 4
                    if kind < 2:
                        dst = WT['tR' if kind == 0 else 'tK'][:, half * 512:(half + 1) * 512]
                        k.cp('scalar', dst, Bk[1][:], r=[bk(1)], w=['tR' if kind == 0 else 'tK'])
                    else:
                        k.cp('scalar', BT_['Vb'][:, half * 512:(half + 1) * 512], Bk[1][:], r=[bk(1)], w=['Vb'])
            elif blk == 24:
                k.stt('vector', t1[:], t1[:], mu[:, blk:blk + 1], zs[:, 1:129], ALU.mult, ALU.add, r=[tk, zk, C], w=[tk])
                k.act(tw[:], t1[:], AF.Tanh, r=[tk], w=['tw'])
            elif blk == 25:
                k.stt('vector', ta[:], t1[:], mu[:, blk:blk + 1], zs[:, 1:129], ALU.mult, ALU.add, r=[tk, zk, C], w=['ta'])
            else:
                k.stt('vector', t1[:], t1[:], mu[:, blk:blk + 1], zs[:, 1:129], ALU.mult, ALU.add, r=[tk, zk, C], w=[tk])
                k.act(tg[:], t1[:], AF.Sigmoid, r=[tk], w=['tg'])
        if RW_STOP <= 2:
            return
        tR, tK, tKK, tSW, tB, tmp, tA, tG, tY = [WT[n] for n in ['tR', 'tK', 'tKK', 'tSW', 'tB', 'tmp', 'tA', 'tG', 'tY']]
        tKD = tK
        P0 = 64 * d

        def lora(lhs, rhs_w, brow_p, dst, dkey, func, brow=brow):
            for half in range(2):
                cs = slice(half * 512, (half + 1) * 512)
                k.mm(wide[half][:], lhs, rhs_w[:, cs] if rhs_w is g2b else rhs_w[lhs_p0:lhs_p0 + 64, cs], True, brow_p is None,
                     r=['tw', 'ta', 'tg', C, 'g2b'], w=[bk(3 + half)])
                if brow_p is not None:
                    k.mm(wide[half][:], ones_f[brow_p:brow_p + 1, :], brow[brow_p:brow_p + 1, cs], False, True, r=['const', C], w=[bk(3 + half)])
                k.act(dst[:, cs], wide[half][:], func, r=[bk(3 + half)], w=[dkey])
        lhs_p0 = P0
        lora(tw[P0:P0 + 64, :], w2s, P0, tSW, 'tSW', AF.Sigmoid)
        lora(ta[P0:P0 + 64, :], a2s, P0, tA, 'tA', AF.Sigmoid, brow2)
        k.tt('gpsimd', tKK[:], tK[:], rwv[:, 0, :], ALU.mult, r=['tK', C], w=['tKK'])
        k.act(tmp[:], tKK[:], AF.Square, r=['tKK'], w=['tmp'])
        ss = st16[0]
        k.p.op('vector', lambda e: e.tensor_reduce(out=ss[:], in_=v3(tmp, 16, 64), axis=AX.X, op=ALU.add), r=['tmp'], w=['ss'])
        k.act(ss[:], ss[:], AF.Sqrt, r=['ss'], w=['ss'])
        k.ts('vector', ss[:], ss[:], 1e-12, None, ALU.max, None, r=['ss'], w=['ss'])
        k.recip(ss[:], ss[:], r=['ss'], w=['ss'])
        k.tt('vector', v3(tKK, 16, 64), v3(tKK, 16, 64), ss[:].rearrange("p (h o) -> p h o", o=1).broadcast_to([128, 16, 64]), ALU.mult,
             r=['tKK', 'ss'], w=['tKK'])
        k.tt('gpsimd', tmp[:], tK[:], rwv[:, 1, :], ALU.mult, r=['tK', C], w=['tmp'])
        if bwd:
            lhs_p0 = 0
            lora(ta[0:64, :], a2s, 0, tB, 'tB', AF.Sigmoid, brow2)
            lhs_p0 = P0
            k.tt('vector', tB[:], tB[:], tA[:], ALU.add, r=['tB', 'tA'], w=['tB'])
            k.stt('vector', tB[:], tB[:], -2.0, tmp[:], ALU.add, ALU.mult, r=['tB', 'tmp'], w=['tB'])
            k.stt('vector', tB[:], tK[:], 2.0, tB[:], ALU.mult, ALU.add, r=['tB', 'tK'], w=['tB'])
            k.tt('gpsimd', tB[:], tB[:], tR[:], ALU.mult, r=['tB', 'tR'], w=['tB'])
            k.tt('gpsimd', tB[:], tB[:], rwvb[:, 0, :], ALU.mult, r=['tB', C], w=['tB'])
            coef = st16[1]
            k.p.op('vector', lambda e: e.tensor_reduce(out=coef[:], in_=v3(tB, 16, 64), axis=AX.X, op=ALU.add), r=['tB'], w=['coef'])
            lora(tg[:], g2b, None, tG, 'tG', AF.Identity)
        k.stt('vector', tmp[:], tA[:], -1.0, tmp[:], ALU.add, ALU.mult, r=['tA', 'tmp'], w=['tmp'])
        k.tt('gpsimd', tK[:], tmp[:], tK[:], ALU.add, r=['tmp', 'tK', 'tB'], w=['tK'])
        k.stt('vector', tB[:], tA[:], -1.0, tKK[:], ALU.mult, ALU.mult, r=['tA', 'tKK', 'coef'], w=['tB'])
        tE = tA
        for half in range(2):
            cs = slice(half * 512, (half + 1) * 512)
            k.mm(wide[half][:], tri[:, d, :], tSW[:, cs], True, True, r=[C, 'tSW'], w=[bk(3 + half)])
            k.act(tE[:, cs], wide[half][:], AF.Exp, r=[bk(3 + half), 'tB'], w=['tE'], scale=-CDEC)
            k.tt('vector', BT_['RTm'][:, cs], tR[:, cs], tE[:, cs], ALU.mult, r=['tR', 'tE'], w=['RTm'])
            k.act(tE[:, cs], wide[half][:], AF.Exp, r=[bk(3 + half), 'RTm'], w=['tE'], scale=CDEC)
            k.tt('vector', BT_['KTm'][:, cs], tKD[:, cs], tE[:, cs], ALU.mult, r=['tK', 'tE'], w=['KTm'])
            k.tt('gpsimd', BT_['BTm'][:, cs], tB[:, cs], tE[:, cs], ALU.mult, r=['tB', 'tE'], w=['BTm'])
            k.tt('vector', tmp[:, cs], wide[half][:], tSW[:, cs], ALU.subtract, r=[bk(3 + half), 'tSW'], w=['tmp'])
            k.act(tE[:, cs], tmp[:, cs], AF.Exp, r=['tmp', 'KTm', 'BTm'], w=['tE'], scale=-CDEC)
            k.tt('vector', BT_['BEm'][:, cs], tKK[:, cs], tE[:, cs], ALU.mult, r=['tKK', 'tE'], w=['BEm'])
            k.mm(wide[half][:], tri[:, 3 - d, :], tSW[:, cs], True, True, r=[C, 'tSW'], w=[bk(3 + half)])
            k.act(tE[:, cs], wide[half][:], AF.Exp, r=[bk(3 + half), 'BEm'], w=['tE'], scale=-CDEC)
            k.tt('vector', BT_['KHm'][:, cs], tKD[:, cs], tE[:, cs], ALU.mult, r=['tK', 'tE'], w=['KHm'])
            k.tt('gpsimd', BT_['BHm'][:, cs], tB[:, cs], tE[:, cs], ALU.mult, r=['tB', 'tE'], w=['BHm'])
        for g8 in range(2):
            pspc = v3(wide[g8], 8, 64)
            for hh in range(8):
                h = g8 * 8 + hh
                k.mm(pspc[0:64, hh, :], tSW[:, h * 64:(h + 1) * 64], ones_f[:, 0:64], True, True, r=['tSW', 'const'], w=[bk(3 + g8)])
            k.act(PCx[:, g8 * 8:(g8 + 1) * 8, :], pspc[0:64, :, :], AF.Exp, r=[bk(3 + g8)], w=['PCx'], scale=-CDEC)
        psx = Bb[:, 0:512].rearrange("p (a b) -> p a b", b=128)
        for h in range(16):
            for q, n in enumerate(['BEm', 'RTm', 'KTm', 'BTm']):
                k.tr(psx[0:64, q, :], BT_[n][:, h * 64:(h + 1) * 64], identb[:], r=[n, C], w=[bk(7)])
            k.cp('scalar' if h % 2 else 'vector', XT[:, h, :, :], psx[0:64, :, :], r=[bk(7)], w=[('XT', h)])

    def heads(c, d, bwd, b4):
        Vb = BT_['Vb']
        M1 = Bk[5][:].rearrange("p (a b) -> p a b", b=256); M2 = Bk[6][:].rearrange("p (a b) -> p a b", b=256)
        idb = ident_f[:].rearrange("p (o n) -> p o n", o=1).broadcast_to([128, 4, 128])
        for q in range(2):
            for hf in range(2):
                h = b4 * 4 + 2 * q + hf
                k.mm(M1[:, hf, :], XT[:, h, 2, :], XT[:, h, 0:2, :], True, True, r=[('XT', h)], w=[bk(5)])
                k.mm(M2[:, hf, :], XT[:, h, 3, :], XT[:, h, 0:2, :], True, True, r=[('XT', h)], w=[bk(6)])
            k.tt('vector', SB1[:, q, :, :], M1, mjt[d][:].rearrange("p a b c -> p a (b c)"), ALU.mult, r=[bk(5), C], w=[('SB1', q)])
            k.tt('vector', SB2[:, q, :, :], M2, mjt[d][:].rearrange("p a b c -> p a (b c)"), ALU.mult, r=[bk(6), C], w=[('SB2', q)])
            k.tt('vector', AdT[:, 2 * q:2 * q + 2, :], M2[:, :, 0:128], mdjt[d][:, 0:2, :], ALU.mult, r=[bk(6), C], w=['PT0'])
            k.tt('vector', NT[:, 2 * q:2 * q + 2, :], M2[:, :, 0:128], mnjt[d][:, 0:2, :], ALU.mult, r=[bk(6), C], w=['NT'])
        if RW_H <= 1:
            return
        P3 = v3(Bk[2], 4, 128); P4 = v3(Bk[4], 4, 128); P5 = v3(Bk[1], 4, 128)
        for hl in range(4):
            h = b4 * 4 + hl
            k.mm(P3[:, hl, :], XT[:, h, 0, :], XT[:, h, 3, :], True, True, r=[('XT', h)], w=[bk(2)])
        k.tt('vector', Pf[0][:], P3, mdtj[d][:], ALU.mult, r=[bk(2), C], w=['P0'])
        if RW_H <= 2:
            return
        PX = v3(Bk[3], 4, 64)
        PXb = Bk[3][:, 256:512].rearrange("p (a b) -> p a b", b=64)
        for hl in range(4):
            hf = hl % 2; h = b4 * 4 + hl
            k.mm(PX[:, hl, :], XT[:, h, 0, :], Tb[:, h, :], True, True, r=[('XT', h), 'Tb'], w=[bk(3)])
            k.mm(PXb[:, hl, :], SB1[:, hl // 2, hf, 0:128], Vb[:, h * 64:(h + 1) * 64], True, True, r=[('SB1', hl // 2), 'Vb'], w=[bk(3)])
        k.cp('scalar', W32[:], PX, r=[bk(3)], w=['W32'])
        k.tt('vector', W32[:], W32[:], PXb, ALU.add, r=[bk(3), 'W32'], w=['W32'])
        if RW_H <= 3:
            return
        k.tt('vector', FT[0][:], PTf[0][:], idb, ALU.add, r=['PT0', 'const'], w=['FT0'])
        if RW_H <= 4:
            return
        for i in range(3):
            pk_, ptk_ = 'P%d' % (i % 2), 'PT%d' % (i % 2)
            npk, nptk = 'P%d' % ((i + 1) % 2), 'PT%d' % ((i + 1) % 2)
            for hl in range(4):
                k.mm(P3[:, hl, :], PTf[i % 2][:, hl, :], Pf[i % 2][:, hl, :], True, True, r=[pk_, ptk_], w=[bk(2)])
            if i < 2:
                for hl in range(4):
                    k.mm(P4[:, hl, :], Pf[i % 2][:, hl, :], PTf[i % 2][:, hl, :], True, True, r=[pk_, ptk_], w=[bk(4)])
            k.tt('vector', IP[:], P3, idb, ALU.add, r=[bk(2), 'const'], w=['IP'])
            if i < 2:
                k.cp('scalar', Pf[(i + 1) % 2][:], P3, r=[bk(2)], w=[npk])
                k.cp('scalar', PTf[(i + 1) % 2][:], P4, r=[bk(4)], w=[nptk])
            for hl in range(4):
                k.mm(P5[:, hl, :], IP[:, hl, :], FT[i % 2][:, hl, :], True, True, r=['IP', 'FT%d' % (i % 2)], w=[bk(1)])
            k.cp('vector' if i % 2 else 'scalar', FT[(i + 1) % 2][:], P5, r=[bk(1)], w=['FT%d' % ((i + 1) % 2)])
        if RW_H <= 5:
            return
        FTf = FT[1]; ftk = 'FT1'
        for hl in range(4):
            k.mm(PX[:, hl, :], FTf[:, hl, :], W32[:, hl, :], True, True, r=[ftk, 'W32'], w=[bk(3)])
        k.cp('scalar', U32[:], PX, r=[bk(3)], w=['U32'])
        for m in range(7):
            for hl in range(4):
                k.mm(PX[:, hl, :], NT[:, hl, :], U32[:, hl, :], True, True, r=['NT', 'U32'], w=[bk(3)])
            k.tt('vector', V32[:], PX, W32[:], ALU.add, r=[bk(3), 'W32'], w=['V32'])
            for hl in range(4):
                k.mm(PX[:, hl, :], FTf[:, hl, :], V32[:, hl, :], True, True, r=[ftk, 'V32'], w=[bk(3)])
            k.cp('scalar', U32[:], PX, r=[bk(3)], w=['U32'])
        k.cp('vector', Xb[:], U32[:], r=['U32'], w=['Xb'])
        if RW_H <= 6:
            return
        PY = v3(Bk[4], 4, 64)
        PYb = Bk[4][:, 256:512].rearrange("p (a b) -> p a b", b=64)
        for hl in range(4):
            hf = hl % 2; h = b4 * 4 + hl
            k.mm(PY[:, hl, :], XT[:, h, 1, :], Tb[:, h, :], True, True, r=[('XT', h), 'Tb'], w=[bk(4)])
            k.mm(PYb[:, hl, :], SB1[:, hl // 2, hf, 128:256], Vb[:, h * 64:(h + 1) * 64], True, False, r=[('SB1', hl // 2), 'Vb'], w=[bk(4)])
            k.mm(PYb[:, hl, :], SB2[:, hl // 2, hf, 128:256], Xb[:, hl, :], False, True, r=[('SB2', hl // 2), 'Xb'], w=[bk(4)])
        k.cp('scalar', WT['tY'][:, b4 * 256:(b4 + 1) * 256], Bk[4][:, 0:256], r=[bk(4)], w=['tY'])
        k.tt('vector', WT['tY'][:, b4 * 256:(b4 + 1) * 256], WT['tY'][:, b4 * 256:(b4 + 1) * 256], Bk[4][:, 256:512], ALU.add, r=[bk(4), 'tY'], w=['tY'])
        if RW_H <= 7:
            return
        PT_ = v3(Bk[3], 4, 64)
        for hl in range(4):
            h = b4 * 4 + hl
            k.mm(PT_[0:64, hl, :], BT_['KHm'][:, h * 64:(h + 1) * 64], Vb[:, h * 64:(h + 1) * 64], True, False, r=['KHm', 'Vb'], w=[bk(3)])
            k.mm(PT_[0:64, hl, :], BT_['BHm'][:, h * 64:(h + 1) * 64], Xb[:, hl, :], False, True, r=['BHm', 'Xb'], w=[bk(3)])
        ts_ = Tst[:, b4 * 4:(b4 + 1) * 4, :]
        k.tt('gpsimd', ts_, ts_, PCx[:, b4 * 4:(b4 + 1) * 4, :], ALU.mult, r=['Tst', 'PCx'], w=['Tst'])
        k.tt('vector', ts_, ts_, PT_[0:64, :, :], ALU.add, r=['Tst', bk(3)], w=['Tst'])
        k.cp('scalar', Tb[:, b4 * 4:(b4 + 1) * 4, :], ts_, r=['Tst'], w=['Tb'])

    def readout(c):
        tY = WT['tY']; tYF = WT['tSW']; tmp = WT['tmp']; tG = WT['tG']
        t0 = 128 * c
        k.dma(tYF[:], yf_dram[t0:t0 + 128, :], w=['tSW'])
        k.tt('vector', tY[:], tY[:], tYF[:], ALU.add, r=['tY', 'tSW'], w=['tY'])
        s1, s2, coef = st16[2], st16[3], st16[1]
        k.p.op('vector', lambda e: e.tensor_reduce(out=s1[:], in_=v3(tY, 16, 64), axis=AX.X, op=ALU.add), r=['tY'], w=['s1'])
        k.act(tmp[:], tY[:], AF.Square, r=['tY'], w=['tmp'])
        k.p.op('vector', lambda e: e.tensor_reduce(out=s2[:], in_=v3(tmp, 16, 64), axis=AX.X, op=ALU.add), r=['tmp'], w=['s2'])
        k.ts('vector', s1[:], s1[:], 1.0 / 64, None, ALU.mult, None, r=['s1'], w=['s1'])
        k.tt('vector', st16[0][:], s1[:], s1[:], ALU.mult, r=['s1'], w=['ss'])
        k.stt('vector', s2[:], s2[:], 1.0 / 64, st16[0][:], ALU.mult, ALU.subtract, r=['s2', 'ss'], w=['s2'])
        k.act(s2[:], s2[:], AF.Sqrt, r=['s2'], w=['s2'], bias=RW_GN_EPS)
        k.recip(s2[:], s2[:], r=['s2'], w=['s2'])
        bc = lambda t: t[:].rearrange("p (h o) -> p h o", o=1).broadcast_to([128, 16, 64])
        k.tt('vector', v3(tY, 16, 64), v3(tY, 16, 64), bc(s1), ALU.subtract, r=['tY', 's1'], w=['tY'])
        k.tt('vector', v3(tY, 16, 64), v3(tY, 16, 64), bc(s2), ALU.mult, r=['tY', 's2'], w=['tY'])
        k.tt('gpsimd', tY[:], tY[:], rwvb[:, 1, :], ALU.mult, r=['tY', C], w=['tY'])
        k.tt('gpsimd', tY[:], tY[:], rwvb[:, 2, :], ALU.add, r=['tY', C], w=['tY'])
        k.tt('vector', v3(tmp, 16, 64), v3(BT_['Vb'], 16, 64), bc(coef), ALU.mult, r=['Vb', 'coef', 'tmp'], w=['tmp'])
        k.tt('vector', tY[:], tY[:], tmp[:], ALU.add, r=['tY', 'tmp'], w=['tY'])
        k.tt('vector', tY[:], tY[:], tG[:], ALU.mult, r=['tY', 'tG'], w=['tY'])
        for half in range(2):
            pst = v3(Bk[1], 4, 128)
            for q in range(4):
                blk = half * 4 + q
                k.tr(pst[:, q, :], tY[:, blk * 128:(blk + 1) * 128], ident_f[:], r=['tY', 'const'], w=[bk(1)])
            k.cp('scalar', v3(BT_['RTm'], 8, 128)[:, half * 4:(half + 1) * 4, :], pst, r=[bk(1)], w=['RTm'])
        k.dma(fm(orwT)[:, :, t0:t0 + 128], v3(BT_['RTm'], 8, 128), r=['RTm'])

    for d in range(2):
        k.memset('vector', Tst[:], 0.0, w=['Tst'])
        k.memset('gpsimd', Tb[:], 0.0, w=['Tb'])
        for c in (CH_F if d == 0 else CH_B)[:RW_LIMIT]:
            if RW_STOP <= 1:
                continue
            prep(c, d, d == 1)
            if RW_STOP <= 3:
                continue
            for b4 in range(4):
                heads(c, d, d == 1, b4)
            if RW_STOP <= 4:
                continue
            if d == 0:
                k.dma(yf_dram[128 * c:128 * c + 128, :], WT['tY'][:], r=['tY'])
            else:
                readout(c)
        if d == 0:
            p.barrier()
    p.pop()


L_ALL = 4
WSPEC = dict(w_mod=[1024, 6144], b_modT=[128, 48], nrm=[128, 4, 8], w_in=[1024, IN_COLS], cv_w=[128, 8, 31], cv_v=[128, 3, 8],
             at_g=[128, 2], w_branch=[3072, 1024], w_out=[1024, 1024], w_ff1=[1024, 4096], w_ff2=[4096, 1024],
             rw_g2=[128, 1024], rw_w2=[128, 1024], rw_a2=[128, 1024], rw_w0=[2, 1024], rw_a0=[2, 1024], rwv=[128, 5, 1024], rw_mu=[128, 27])
CSPEC = dict(ones_f=[128, 128], ident_f=[128, 128], rsw=[128, 128], tri=[128, 5, 128], cosT=[128, 2048], sinT=[128, 2048])


def build_program(n_layers=L_ALL, stages='all'):
    nc = bass.Bass("TRN2", target_bir_lowering=False)

    def din(name, shape, dt=F32):
        return nc.dram_tensor(name, list(shape), dt, kind="ExternalInput").ap()

    G = {}
    xT = din('xT', [1024, T])
    G['cT'] = din('cT', [128, 8, 2])
    cd = {n: din(n, s) for n, s in CSPEC.items()}
    wd = {n: din(n, [n_layers] + s) for n, s in WSPEC.items()}
    xo = nc.dram_tensor('xo', [1024, T], F32, kind="ExternalOutput").ap()
    hT = nc.dram_tensor('hT_s', [1024, HP], BF16, kind="Internal").ap()
    orwT = nc.dram_tensor('orwT_s', [1024, T], BF16, kind="Internal").ap()
    ocvT = nc.dram_tensor('ocvT_s', [1024, T], BF16, kind="Internal").ap()
    oatT = nc.dram_tensor('oatT_s', [1024, T], BF16, kind="Internal").ap()
    yf = nc.dram_tensor('yf_s', [T, 1024], F32, kind="Internal").ap()
    G['tri'] = cd['tri']; G['cosT'] = cd['cosT']; G['sinT'] = cd['sinT']
    with nc.Block() as block:
        p = Prog(nc, block)
        k = K(p)
        G['mv'] = p.sb('mv', [128, 6, 8, 2], F32)
        for n_ in ('ones_f', 'ident_f', 'rsw'):
            G[n_] = p.sb(n_, CSPEC[n_], F32)
            k.dma(G[n_][:], cd[n_], w=['const'])
        k.dma(xo, xT)
        p.barrier()
        for l in range(n_layers):
            W = {n: wd[n][l] for n in WSPEC}
            stage_mod(k, G, W)
            stage_prenorm(k, G, xo, hT)
            stage_conv(k, G, W, hT, ocvT)
            stage_attn(k, G, W, hT, oatT)
            stage_rwkv(k, G, W, hT, yf, orwT)
            stage_merge(k, G, W, hT, orwT, ocvT, oatT, xo, xo)
            stage_mlp(k, G, W, xo)
        p.finish()
    return nc


def _featT(v, nb):
    return np.ascontiguousarray(np.asarray(v).reshape(nb, 128).T)


def host_inputs(inputs, n_layers=L_ALL, cores=range(8)):
    g = {k_: np.asarray(v) for k_, v in inputs.items()}
    Ls = range(n_layers)
    wm = {}
    wm['w_mod'] = g['w_mod'][:n_layers]
    wm['b_modT'] = np.stack([_featT(g['b_mod'][l], 48) for l in Ls])
    wm['nrm'] = np.stack([np.stack([_featT(g[n][l], 8) for n in ('norm_mix_pre', 'norm_mix_post', 'norm_mlp_pre', 'norm_mlp_post')], 1) for l in Ls])
    wm['w_in'] = g['w_in'][:n_layers]
    wm['cv_w'] = np.stack([g['cv_dw_w'][l].T.reshape(8, 128, 31).transpose(1, 0, 2) for l in Ls])
    wm['cv_v'] = np.stack([np.stack([_featT(g[n][l], 8) for n in ('cv_dw_b', 'cv_ln_g', 'cv_ln_b')], 1) for l in Ls])
    wm['at_g'] = np.stack([np.stack([g['at_q_norm'][l], g['at_k_norm'][l]], 1) for l in Ls])
    wm['w_branch'] = g['w_branch'][:n_layers].reshape(n_layers, 3072, 1024)
    wm['w_out'] = g['w_out'][:n_layers]
    wm['w_ff1'] = g['w_ff1'][:n_layers]
    wm['w_ff2'] = g['w_ff2'][:n_layers]
    wm['rw_g2'] = g['rw_g2'][:n_layers]
    wm['rw_w2'] = g['rw_w2'][:n_layers].reshape(n_layers, 128, 1024)
    wm['rw_a2'] = g['rw_a2'][:n_layers].reshape(n_layers, 128, 1024)
    wm['rw_w0'] = g['rw_w0'][:n_layers]
    wm['rw_a0'] = g['rw_a0'][:n_layers]
    wm['rwv'] = np.stack([np.broadcast_to(np.stack([g['rw_k_k'][l], g['rw_k_a'][l], g['rw_r_k'][l].reshape(-1), g['rw_ln_g'][l], g['rw_ln_b'][l]], 0)[None],
                                          (128, 5, 1024)) for l in Ls])
    wm['rw_mu'] = np.stack([_featT(g['rw_mu'][l], 27) for l in Ls])
    wm = {k_: np.ascontiguousarray(v, dtype=np.float32) for k_, v in wm.items()}
    ii = np.arange(128)[:, None]; tt = np.arange(128)[None, :]
    R = np.zeros((128, 128), np.float32)
    for i in range(64):
        R[2 * i, 2 * i + 1] = -1.0; R[2 * i + 1, 2 * i] = 1.0
    row = np.repeat(np.arange(32), 64).astype(np.float32); colp = np.tile(np.arange(64), 32).astype(np.float32)
    freqs = (10000.0 ** (-np.arange(0, 64, 2, dtype=np.float32) / 64)).astype(np.float32)
    ang = np.concatenate([row[:, None] * freqs, colp[:, None] * freqs], -1)
    cm = dict(ones_f=np.ones((128, 128), np.float32), ident_f=np.eye(128, dtype=np.float32), rsw=np.ascontiguousarray(R.T),
              tri=np.ascontiguousarray(np.stack([(ii <= tt), (ii >= tt), (ii < tt), (ii > tt), (ii // 16 == tt // 16)], 1).astype(np.float32)),
              cosT=np.ascontiguousarray(np.repeat(np.cos(ang).astype(np.float32), 2, axis=1).T),
              sinT=np.ascontiguousarray(np.repeat(np.sin(ang).astype(np.float32), 2, axis=1).T))
    maps = []
    for b in cores:
        m = dict(wm); m.update(cm)
        m['xT'] = np.ascontiguousarray(np.concatenate([g['ctx'][b], g['x'][b]], 0).T.astype(np.float32))
        m['cT'] = np.ascontiguousarray(np.stack([_featT(g['c'][b], 8), _featT(g['c_ctx'], 8)], -1).astype(np.float32))
        maps.append(m)
    return maps


def kernel(**inputs):
    nc = build_program(L_ALL)
    maps = host_inputs(inputs, L_ALL, range(8))
    res = run_bass_kernel_spmd(nc, maps, core_ids=list(range(8)))
    out = np.stack([np.ascontiguousarray(np.asarray(r['xo'])[:, NCTX:].T) for r in res.results], 0)
    return out.astype(np.float32)
```
